# Optimizing a Trainium2 kernel written in Bass

```python
import math
import jax, jax.numpy as jnp
from jax import lax
import numpy as np

D_MODEL = 2048
BATCH = 4
SEQ = 2048
DEPTH = 1

D_RNN = 2048
RNN_BLOCKS = 16
RNN_BLOCK_W = D_RNN // RNN_BLOCKS
CONV_W = 4
LRU_C = 8.0
HEAD_DIM = 128
HEADS_PER_GROUP = 8
DILATED_GROUPS = ((128, 1), (512, 4), (2048, 16))
N_GROUPS = len(DILATED_GROUPS)
N_ATTN_HEADS = N_GROUPS * HEADS_PER_GROUP
D_QKV = N_ATTN_HEADS * HEAD_DIM
D_ATTN_OUT = HEADS_PER_GROUP * HEAD_DIM
ATTN_BLOCK = 128
ROPE_THETA = 500000.0
ROT_DIM = HEAD_DIM // 4
NORM_EPS = 1e-6
SPLIT_SIZES = (D_RNN, D_RNN, D_QKV, D_QKV, D_QKV, D_ATTN_OUT, D_MODEL, D_MODEL)
D_IN_TOTAL = 2 * D_RNN + 3 * D_QKV + D_ATTN_OUT + 2 * D_MODEL

kernel_name = "hybrid_rglru_dilated_attn_block"


def rms_norm(x, g):
    xf = x.astype(jnp.float32)
    y = xf * lax.rsqrt(jnp.mean(xf * xf, axis=-1, keepdims=True) + NORM_EPS)
    return (y * g.astype(jnp.float32)).astype(x.dtype)


def split_columns(p):
    offs, acc = [], 0
    for s in SPLIT_SIZES[:-1]:
        acc += s
        offs.append(acc)
    return jnp.split(p, offs, axis=-1)


def partial_rope(t, pos):
    half = ROT_DIM // 2
    inv = jnp.exp(-math.log(ROPE_THETA) * jnp.arange(half, dtype=jnp.float32) * (2.0 / ROT_DIM))
    ang = pos.astype(jnp.float32)[:, None] * inv[None, :]
    cos = jnp.cos(ang)[None, :, None, :]
    sin = jnp.sin(ang)[None, :, None, :]
    tr = t[..., :ROT_DIM].astype(jnp.float32)
    t1, t2 = tr[..., :half], tr[..., half:]
    rot = jnp.concatenate([t1 * cos - t2 * sin, t2 * cos + t1 * sin], axis=-1)
    return jnp.concatenate([rot.astype(t.dtype), t[..., ROT_DIM:]], axis=-1)


def rg_lru(u, conv_w, conv_b, w_rg, b_rg, w_ig, b_ig, lru_lambda):
    B, S, C = u.shape
    xc = lax.conv_general_dilated(
        u, conv_w[:, None, :].astype(u.dtype), window_strides=(1,), padding=[(CONV_W - 1, 0)],
        dimension_numbers=("NWC", "WIO", "NWC"), feature_group_count=C) + conv_b
    xb = xc.reshape(B, S, RNN_BLOCKS, RNN_BLOCK_W)
    r = jax.nn.sigmoid(jnp.einsum('bsnc,ncd->bsnd', xb, w_rg).reshape(B, S, C) + b_rg)
    i = jax.nn.sigmoid(jnp.einsum('bsnc,ncd->bsnd', xb, w_ig).reshape(B, S, C) + b_ig)
    log_a = -LRU_C * r.astype(jnp.float32) * jax.nn.softplus(-lru_lambda.astype(jnp.float32))
    a = jnp.exp(log_a)
    b = jnp.sqrt(-jnp.expm1(2.0 * log_a)) * (i * xc).astype(jnp.float32)

    def combine(left, right):
        a1, b1 = left
        a2, b2 = right
        return a1 * a2, a2 * b1 + b2

    _, h = lax.associative_scan(combine, (a, b), axis=1)
    return h.astype(u.dtype)


def dilated_group_attention(q, k, v, window, dilation):
    B, S, H, D = q.shape
    L = S // dilation
    steps = window // dilation
    n_blk = -(-L // ATTN_BLOCK)
    Lp = n_blk * ATTN_BLOCK

    def to_blocks(t):
        t = t.reshape(B, L, dilation, H, D).transpose(0, 2, 1, 3, 4)
        t = jnp.pad(t, ((0, 0), (0, 0), (0, Lp - L), (0, 0), (0, 0)))
        return t.reshape(B, dilation, n_blk, ATTN_BLOCK, H, D)

    def with_prev(t):
        prev = jnp.pad(t, ((0, 0), (0, 0), (1, 0), (0, 0), (0, 0), (0, 0)))[:, :, :-1]
        return jnp.concatenate([prev, t], axis=3)

    qb = to_blocks(q).astype(jnp.float32)
    kk = with_prev(to_blocks(k)).astype(jnp.float32)
    vv = with_prev(to_blocks(v)).astype(jnp.float32)
    s = jnp.einsum('brnqhd,brnkhd->brnhqk', qb, kk) * (D ** -0.5)
    qi = jnp.arange(ATTN_BLOCK)[:, None]
    kj = jnp.arange(2 * ATTN_BLOCK)[None, :]
    dist = qi + ATTN_BLOCK - kj
    key_pos = jnp.arange(n_blk)[:, None, None] * ATTN_BLOCK + kj[None] - ATTN_BLOCK
    valid = (dist >= 0)[None] & (dist <= steps)[None] & (key_pos >= 0)
    s = jnp.where(valid[None, None, :, None], s, -1e30)
    m = jnp.max(s, axis=-1)
    p = jnp.exp(s - m[..., None])
    l = jnp.sum(p, axis=-1)
    o = jnp.einsum('brnhqk,brnkhd->brnqhd', p, vv) / jnp.swapaxes(l, -1, -2)[..., None]
    log_den = jnp.swapaxes(m + jnp.log(l), -1, -2)

    def from_blocks(t):
        rest = t.shape[4:]
        t = t.reshape((B, dilation, Lp) + rest)[:, :, :L]
        t = jnp.moveaxis(t, 1, 2)
        return t.reshape((B, S) + rest)

    return from_blocks(o), from_blocks(log_den)


def setup_inputs(seed: int = 0) -> dict:
    key = jax.random.key(seed)
    ks = jax.random.split(key, 16)
    f32 = jnp.float32
    nrm = lambda k, shape, scale: jax.random.normal(k, shape, f32) * scale
    u = jax.random.uniform(ks[8], (DEPTH, D_RNN), f32, 0.9, 0.999)
    s = u ** (1.0 / LRU_C)
    lru_lambda = jnp.log(s) - jnp.log1p(-s)
    return {
        "x": jax.random.normal(ks[0], (BATCH, SEQ, D_MODEL), f32),
        "ln_pre_g": 1.0 + nrm(ks[1], (DEPTH, D_MODEL), 0.02),
        "w_in": nrm(ks[2], (DEPTH, D_MODEL, D_IN_TOTAL), D_MODEL ** -0.5),
        "conv_w": nrm(ks[3], (DEPTH, CONV_W, D_RNN), CONV_W ** -0.5),
        "conv_b": nrm(ks[4], (DEPTH, D_RNN), 0.02),
        "w_rg": nrm(ks[5], (DEPTH, RNN_BLOCKS, RNN_BLOCK_W, RNN_BLOCK_W), RNN_BLOCK_W ** -0.5),
        "b_rg": nrm(ks[6], (DEPTH, D_RNN), 0.02),
        "w_ig": nrm(ks[7], (DEPTH, RNN_BLOCKS, RNN_BLOCK_W, RNN_BLOCK_W), RNN_BLOCK_W ** -0.5),
        "b_ig": nrm(ks[9], (DEPTH, D_RNN), 0.02),
        "lru_lambda": lru_lambda,
        "w_rnn_out": nrm(ks[10], (DEPTH, D_RNN, D_MODEL), D_RNN ** -0.5),
        "w_attn_out": nrm(ks[11], (DEPTH, D_ATTN_OUT, D_MODEL), D_ATTN_OUT ** -0.5),
        "w_o": nrm(ks[12], (DEPTH, D_MODEL, D_MODEL), D_MODEL ** -0.5),
        "ln_post_g": 1.0 + nrm(ks[13], (DEPTH, D_MODEL), 0.02),
    }


def reference(x, ln_pre_g, w_in, conv_w, conv_b, w_rg, b_rg, w_ig, b_ig, lru_lambda,
              w_rnn_out, w_attn_out, w_o, ln_post_g):
    B, S, _ = x.shape
    pos = jnp.arange(S)
    for layer in range(DEPTH):
        h = rms_norm(x, ln_pre_g[layer])
        proj = jnp.einsum('bsd,de->bse', h, w_in[layer])
        rnn_x, rnn_gate, q, k, v, attn_gate, g_rnn, g_attn = split_columns(proj)

        hr = rg_lru(rnn_x, conv_w[layer], conv_b[layer], w_rg[layer], b_rg[layer],
                    w_ig[layer], b_ig[layer], lru_lambda[layer])
        y_rnn = jnp.einsum('bsc,cd->bsd', hr * jax.nn.silu(rnn_gate), w_rnn_out[layer])

        q = partial_rope(q.reshape(B, S, N_ATTN_HEADS, HEAD_DIM), pos)
        k = partial_rope(k.reshape(B, S, N_ATTN_HEADS, HEAD_DIM), pos)
        v = v.reshape(B, S, N_ATTN_HEADS, HEAD_DIM)
        q = q.reshape(B, S, N_GROUPS, HEADS_PER_GROUP, HEAD_DIM)
        k = k.reshape(B, S, N_GROUPS, HEADS_PER_GROUP, HEAD_DIM)
        v = v.reshape(B, S, N_GROUPS, HEADS_PER_GROUP, HEAD_DIM)
        outs, dens = [], []
        for g, (window, dilation) in enumerate(DILATED_GROUPS):
            o_g, d_g = dilated_group_attention(q[:, :, g], k[:, :, g], v[:, :, g], window, dilation)
            outs.append(o_g)
            dens.append(d_g)
        alpha = jax.nn.softmax(jnp.stack(dens, axis=0), axis=0)
        o = jnp.sum(alpha[..., None] * jnp.stack(outs, axis=0), axis=0)
        o = o.reshape(B, S, D_ATTN_OUT).astype(x.dtype)
        y_attn = jnp.einsum('bsc,cd->bsd', o * jax.nn.silu(attn_gate), w_attn_out[layer])

        merged = jax.nn.sigmoid(g_rnn) * y_rnn + jax.nn.sigmoid(g_attn) * y_attn
        y = jnp.einsum('bsd,de->bse', merged, w_o[layer])
        x = x + rms_norm(y, ln_post_g[layer])
    return x
```

```python
import contextlib
import math
import os

import numpy as np
import ml_dtypes

import concourse.bass as bass
import concourse.mybir as mybir
from concourse.bass_utils import run_bass_kernel_spmd

F32 = mybir.dt.float32
BF16 = mybir.dt.bfloat16
AF = mybir.ActivationFunctionType
ALU = mybir.AluOpType

D = 2048
NK = 16
TOK = 2048
OWN = 1024
DIN = 18432
NEG = -30000.0
EPS = 1e-6
DIL = (1, 4, 16)
HIST = (128, 512, 1024)
SCALE = 1.0 / math.sqrt(128.0)
PW = 256
NSLOT = 4

V_GPRE, V_CW, V_CB, V_BRG, V_BIG, V_LAM, V_HM, V_N = 0, 16, 80, 96, 112, 128, 144, 145
M_A0, M_A1, M_G1, M_G2, M_N = 0, 256, 1280, 2304, 2816

ENGS = ("pe", "act", "dve", "pool", "sp")


class Buf:
    def __init__(self, name):
        self.name = name
        self.w = {}
        self.r = {}


class Sched:
    def __init__(self):
        self.ops = {e: [] for e in ENGS}
        self.cnt = {}
        self.waited = {}
        self.semnames = set("p_" + e for e in ENGS)
        self.limit = int(os.environ.get("KLIM", "0")) or None
        self.n = 0
        self.log = []

    def _need(self, eng, deps, sem, count, is_dma):
        if self.waited.get((eng, sem), 0) >= count:
            return
        deps[sem] = max(deps.get(sem, 0), count)

    def emit(self, eng, fns, reads=(), writes=(), dma=None, force=False):
        self.n += 1
        if self.limit is not None and self.n > self.limit and not force:
            return ("none", 0)
        self.log.append((self.n, eng, [b.name for b in reads], [b.name for b in writes]))
        deps = {}
        isd = dma is not None
        for b in reads:
            for k, v in b.w.items():
                self._need(eng, deps, k, v, isd)
        for b in writes:
            for k, v in b.w.items():
                self._need(eng, deps, k, v, isd)
            for k, v in b.r.items():
                self._need(eng, deps, k, v, isd)
        for k, v in deps.items():
            self.waited[(eng, k)] = v
        if isd:
            sem, inc = dma, 16
            self.semnames.add(dma)
        else:
            sem, inc = "p_" + eng, 1
        self.cnt[sem] = self.cnt.get(sem, 0) + inc
        c = self.cnt[sem]
        for b in writes:
            b.w = {sem: c}
            b.r = {}
        for b in reads:
            b.r[sem] = max(b.r.get(sem, 0), c)
        if not isinstance(fns, list):
            fns = [fns]
        self.ops[eng].append((sorted(deps.items()), fns, sem, inc))
        return (sem, c)

    def replay(self, eng, e, sems):
        for deps, fns, sem, inc in self.ops[eng]:
            for k, v in deps:
                e.wait_ge(sems[k], v)
            for f in fns[:-1]:
                f(e)
            fns[-1](e).then_inc(sems[sem], inc)


def MM(out, lhsT, rhs, start, stop, skip=False):
    if skip:
        return lambda e: e.matmul(out, lhsT, rhs, start=start, stop=stop, skip_group_check=True)
    return lambda e: e.matmul(out, lhsT, rhs, start=start, stop=stop)


def ACT(out, in_, func, bias=None, scale=None, accum=None):
    kw = {}
    if bias is not None:
        kw["bias"] = bias
    if scale is not None:
        kw["scale"] = scale
    if accum is not None:
        kw["accum_out"] = accum
    return lambda e: e.activation(out=out, in_=in_, func=func, **kw)


def TT(out, a, b, op):
    return lambda e: e.tensor_tensor(out=out, in0=a, in1=b, op=op)


def TS(out, a, s1, op0, s2=None, op1=None):
    if op1 is None:
        return lambda e: e.tensor_scalar(out=out, in0=a, scalar1=s1, scalar2=None, op0=op0)
    return lambda e: e.tensor_scalar(out=out, in0=a, scalar1=s1, scalar2=s2, op0=op0, op1=op1)


def CP(out, a):
    return lambda e: e.tensor_copy(out=out, in_=a)


def sl(start, n, step):
    return slice(start, start + (n - 1) * step + 1, step)


def DMA(out, in_):
    return lambda e: e.dma_start(out=out, in_=in_)


class _Stop(Exception):
    pass


def build_nc(debug=False, stop=9):
    nc = bass.Bass("TRN2", target_bir_lowering=False)
    dt = lambda n, s, d, k="ExternalInput": nc.dram_tensor(n, s, d, kind=k).ap()
    x_in = dt("x_in", [TOK, D], F32)
    w_in = dt("w_in", [D, DIN], F32)
    w_ro = dt("w_ro", [D, D], F32)
    w_ao = dt("w_ao", [1024, D], F32)
    w_o = dt("w_o", [D, D], F32)
    w_rg = dt("w_rg", [16, 128, 128], F32)
    w_ig = dt("w_ig", [16, 128, 128], F32)
    vecs_d = dt("vecs", [128, V_N], F32)
    gpost_d = dt("gpost", [128, D], F32)
    cbf_d = dt("cbf", [128, 384], BF16)
    rope_d = dt("rope", [32, 2 * TOK], BF16)
    mask_d = dt("masks", [128, M_N], BF16)
    y_out = dt("y", [OWN, D], F32, "ExternalOutput")
    dbg = {}
    if debug:
        dbg["hT"] = dt("d_hT", [128, NK * TOK], BF16, "ExternalOutput")
        dbg["hgT"] = dt("d_hgT", [128, NK * OWN], BF16, "ExternalOutput")
        dbg["ogT"] = dt("d_ogT", [128, 8 * OWN], BF16, "ExternalOutput")

    S = Sched()
    TEMPB = 37 * 1024
    with contextlib.ExitStack() as es:
        sb = lambda n, s, d: es.enter_context(nc.sbuf_tensor("s_" + n, s, d))
        hT = sb("hT", [128, NK, TOK], BF16)
        hgT = sb("hgT", [128, NK, OWN], BF16)
        ogT = sb("ogT", [128, 8, OWN], BF16)
        ring = sb("ring", [128, NSLOT, NK * PW], BF16)
        cbf = sb("cbf", [128, 384], BF16)
        rope = sb("rope", [32, 2 * TOK], BF16)
        masks = sb("masks", [128, M_N], BF16)
        vecs = sb("vecs", [128, V_N], F32)
        stat = sb("stat", [128, 64], F32)
        diag = sb("diag", [128, 4, 128], BF16)
        tmp = sb("tmp", [128, TEMPB // 4], F32)
        wgT = sb("wgT", [128, 2, 16, 128], BF16)
        ps = es.enter_context(nc.psum_tensor("ps", [128, 8, 512], F32))

        ident = cbf[:, 0:128]
        ones = cbf[:, 128:256]
        pm = cbf[:, 256:384]
        ropeC = rope[:, 0:TOK]
        ropeS = rope[:, TOK:2 * TOK]

        def carve(off, shape, d):
            esz = 2 if d == BF16 else 4
            n = int(np.prod(shape[1:]))
            nb = n * esz
            assert off % 4 == 0 and nb % 4 == 0 and off + nb <= TEMPB, (off, nb)
            v = tmp[:, off // 4:(off + nb) // 4]
            if d != F32:
                v = v.bitcast(d)
            if len(shape) == 3:
                v = v.rearrange("p (a b) -> p a b", a=shape[1])
            return v

        B_hT = [Buf("hT_hist"), Buf("hT_own")]
        B_hg, B_og = Buf("hgT"), Buf("ogT")
        B_const = Buf("const")
        B_stat = Buf("stat")
        B_bank = [Buf("bank%d" % i) for i in range(8)]
        B_slot = [Buf("slot%d" % i) for i in range(NSLOT)]
        bank_ptr = [0]

        def alloc(n):
            p = bank_ptr[0]
            if n > 1 and p % n:
                p += n - p % n
            if p + n > 8:
                p = 0
            bank_ptr[0] = (p + n) % 8
            return p

        def PS(b, n=1, w=None):
            if n == 1:
                return ps[:, b, :] if w is None else ps[:, b, 0:w]
            return ps[:, b:b + n, :].rearrange("p a b -> p (a b)")

        def BB(b, n=1):
            return B_bank[b:b + n]

        S.emit("sp", DMA(vecs[:], vecs_d), writes=[B_const], dma="d_c")
        S.emit("sp", DMA(cbf[:], cbf_d), writes=[B_const], dma="d_c")
        S.emit("sp", DMA(rope[:], rope_d), writes=[B_const], dma="d_c")
        S.emit("sp", DMA(masks[:], mask_d), writes=[B_const], dma="d_c")
        cch = stat[:, 48:64]
        S.emit("act", ACT(stat[:, 32:48], vecs[:, V_LAM:V_LAM + 16], AF.Exp, scale=-1.0),
               reads=[B_const], writes=[B_stat])
        S.emit("act", ACT(stat[:, 32:48], stat[:, 32:48], AF.Ln, bias=1.0), reads=[B_stat], writes=[B_stat])
        S.emit("dve", TS(cch, stat[:, 32:48], -8.0, ALU.mult), reads=[B_stat], writes=[B_stat])

        panels = []

        def wcols(w, c0, nk=NK):
            return (w[0:nk * 128, c0:c0 + PW], nk)

        for cp in range(8):
            panels.append(wcols(w_in, cp * PW))
            panels.append(wcols(w_in, 2048 + cp * PW))
        for hp in range(4):
            for g in range(3):
                for base in (4096, 7168, 10240):
                    panels.append(wcols(w_in, base + g * 1024 + hp * PW))
            panels.append(wcols(w_in, 13312 + hp * PW))
        for mp in range(8):
            panels.append(wcols(w_in, 14336 + mp * PW))
            panels.append(wcols(w_ro, mp * PW))
            panels.append(wcols(w_in, 16384 + mp * PW))
            panels.append(wcols(w_ao, mp * PW, 8))
        n_ring_panels = len(panels)
        pstate = {"issued": 0, "next": 0}

        def slot_ap(i):
            return ring[:, i, :].rearrange("p (k n) -> p k n", n=PW)

        def issue_upto(j):
            while pstate["issued"] <= min(j, n_ring_panels - 1):
                i = pstate["issued"]
                src, nk = panels[i]
                s = i % NSLOT
                S.emit("pool", DMA(slot_ap(s)[:, 0:nk, :], src.rearrange("(k p) n -> p k n", p=128)),
                       writes=[B_slot[s]], dma="d_w%d" % s)
                pstate["issued"] += 1

        def next_panel():
            i = pstate["next"]
            pstate["next"] += 1
            issue_upto(i)
            s = i % NSLOT
            return slot_ap(s), B_slot[s]

        def release():
            issue_upto(pstate["next"] + NSLOT - 1)

        if not os.environ.get("KSKIP_W"):
            release()

        def _build_phases():
            if stop < 0:
                raise _Stop()
            xt = [carve(i * 8192, [128, D], F32) for i in range(2)]
            xs = [carve(16384 + i * 4096, [128, D], BF16) for i in range(4)]
            B_xt = [Buf("xt0"), Buf("xt1")]
            B_xs = [Buf("xs%d" % i) for i in range(4)]
            ev = [0]

            def evac_scaled(out, in_, col, reads, writes):
                ev[0] += 1
                if ev[0] % 2:
                    S.emit("act", ACT(out, in_, AF.Copy, scale=col), reads=reads, writes=writes)
                else:
                    S.emit("dve", TS(out, in_, col, ALU.mult), reads=reads, writes=writes)

            for grp in range(4):
                for j in range(4):
                    t = grp * 4 + j
                    S.emit("sp", DMA(xt[t % 2], x_in[t * 128:(t + 1) * 128, :]), writes=[B_xt[t % 2]],
                           dma="d_x%d" % (t % 2))
                    S.emit("act", ACT(xs[j], xt[t % 2], AF.Square, accum=stat[:, t:t + 1]),
                           reads=[B_xt[t % 2]], writes=[B_xs[j], B_stat])
                    S.emit("act", ACT(stat[:, 16 + t:17 + t], stat[:, t:t + 1], AF.Sqrt, bias=EPS, scale=1.0 / D),
                           reads=[B_stat], writes=[B_stat])
                    S.emit("dve", lambda e, t=t: e.reciprocal(out=stat[:, 16 + t:17 + t], in_=stat[:, 16 + t:17 + t]),
                           reads=[B_stat], writes=[B_stat])
                    S.emit("act", ACT(xs[j], xt[t % 2], AF.Copy, scale=stat[:, 16 + t:17 + t]),
                           reads=[B_xt[t % 2], B_stat], writes=[B_xs[j]])
                for kc in range(NK):
                    b = alloc(1)
                    S.emit("pe", [MM(ps[:, b, j * 128:(j + 1) * 128], xs[j][:, kc * 128:(kc + 1) * 128], ident,
                                     True, True) for j in range(4)],
                           reads=B_xs + [B_const], writes=BB(b))
                    evac_scaled(hT[:, kc, grp * 512:(grp + 1) * 512], PS(b), vecs[:, V_GPRE + kc:V_GPRE + kc + 1],
                                BB(b) + [B_const], [B_hT[grp // 2]])

            def proj_job(slot, bslot, j, rhs_fn, nk, nch, extra_reads):
                b = alloc(nch)
                fns = []
                for k in range(nk):
                    for ch in range(nch):
                        fns.append(MM(ps[:, b + ch, :], slot[:, k, j * 128:(j + 1) * 128], rhs_fn(k, ch),
                                      k == 0, k == nk - 1))
                S.emit("pe", fns, reads=[bslot] + extra_reads, writes=BB(b, nch))
                return b

            if stop < 1:
                raise _Stop()
            o = 0
            wrg = wgT[:, 0, :, :]
            wig = wgT[:, 1, :, :]
            xT = carve(o, [128, 2056], BF16); o += 4112
            xcf = carve(o, [128, OWN], F32); o += 4096
            xcb = carve(o, [128, OWN], BF16); o += 2048
            Ab = carve(o, [128, OWN], F32); o += 4096
            Ib = carve(o, [128, OWN], F32); o += 4096
            Sb = carve(o, [128, OWN], F32); o += 4096
            Hb = carve(o, [128, OWN], F32); o += 4096
            Gb = carve(o, [128, OWN], F32); o += 4096
            B_wg, B_xT, B_xcf, B_xcb = Buf("wg"), Buf("xT"), Buf("xcf"), Buf("xcb")
            B_A, B_I, B_S, B_H, B_G, B_diag, B_carry = (Buf(n) for n in ("A", "I", "S", "H", "G", "diag", "carry"))
            carry = stat[:, 40:41]
            ph0 = B_xt + B_xs
            S.emit("pool", DMA(wrg, w_rg.rearrange("n c d -> c n d")), reads=[], writes=[B_wg] + ph0, dma="d_g")
            S.emit("pool", DMA(wig, w_ig.rearrange("n c d -> c n d")), writes=[B_wg], dma="d_g")
            S.emit("dve", lambda e: e.memset(xT[:, 0:4], 0.0), writes=[B_xT] + ph0)
            first_phase1 = [True]

            for c in range(int(os.environ.get('KNC', '16'))):
                if c % 2 == 0:
                    Xs, BXs = next_panel()
                    Gs, BGs = next_panel()
                j = c % 2
                for tap in range(4):
                    S.emit("dve", TS(diag[:, tap, :], ident, vecs[:, V_CW + c * 4 + tap:V_CW + c * 4 + tap + 1], ALU.mult),
                           reads=[B_const], writes=[B_diag])
                for half in range(int(os.environ.get('KNH', '2'))):
                    c0 = half * OWN
                    b = proj_job(Xs, BXs, j, lambda k, ch: hT[:, k, c0 + ch * 512:c0 + (ch + 1) * 512], NK, 2,
                                 [B_hT[half]])
                    S.emit("act", ACT(xT[:, 4 + c0:4 + c0 + OWN], PS(b, 2), AF.Copy), reads=BB(b, 2), writes=[B_xT])
                    b2 = alloc(2)
                    fns = []
                    for tap in range(4):
                        for ch in range(2):
                            fns.append(MM(ps[:, b2 + ch, :], diag[:, tap, :],
                                          xT[:, c0 + ch * 512 + tap + 1:c0 + ch * 512 + tap + 513], tap == 0, tap == 3))
                    S.emit("pe", fns, reads=[B_diag, B_xT], writes=BB(b2, 2))
                    cb = vecs[:, V_CB + c:V_CB + c + 1]
                    S.emit("act", ACT(xcf, PS(b2, 2), AF.Identity, bias=cb), reads=BB(b2, 2) + [B_const], writes=[B_xcf])
                    S.emit("dve", CP(xcb, xcf), reads=[B_xcf], writes=[B_xcb])
                    br = alloc(2)
                    S.emit("pe", [MM(ps[:, br + ch, :], wrg[:, c, :], xcb[:, ch * 512:(ch + 1) * 512], True, True)
                                  for ch in range(2)], reads=[B_wg, B_xcb], writes=BB(br, 2))
                    bi = alloc(2)
                    S.emit("pe", [MM(ps[:, bi + ch, :], wig[:, c, :], xcb[:, ch * 512:(ch + 1) * 512], True, True)
                                  for ch in range(2)], reads=[B_wg, B_xcb], writes=BB(bi, 2))
                    S.emit("act", ACT(Ab, PS(br, 2), AF.Sigmoid, bias=vecs[:, V_BRG + c:V_BRG + c + 1]),
                           reads=BB(br, 2) + [B_const], writes=[B_A])
                    S.emit("act", ACT(Ib, PS(bi, 2), AF.Sigmoid, bias=vecs[:, V_BIG + c:V_BIG + c + 1]),
                           reads=BB(bi, 2) + [B_const], writes=[B_I])
                    S.emit("act", ACT(Ab, Ab, AF.Exp, scale=cch[:, c:c + 1]), reads=[B_A, B_stat], writes=[B_A])
                    S.emit("dve", TT(Sb, Ab, Ab, ALU.mult), reads=[B_A], writes=[B_S])
                    S.emit("act", ACT(Sb, Sb, AF.Sqrt, bias=1.0000001, scale=-1.0), reads=[B_S], writes=[B_S])
                    if half == 0:
                        S.emit("dve", lambda e: e.scalar_tensor_tensor(out=Ib, in0=Ib, scalar=vecs[:, V_HM:V_HM + 1],
                                                                       in1=xcf, op0=ALU.mult, op1=ALU.mult),
                               reads=[B_I, B_xcf, B_const], writes=[B_I])
                    else:
                        S.emit("dve", TT(Ib, Ib, xcf, ALU.mult), reads=[B_I, B_xcf], writes=[B_I])
                    S.emit("dve", TT(Sb, Sb, Ib, ALU.mult), reads=[B_S, B_I], writes=[B_S])
                    init = 0.0 if half == 0 else carry
                    S.emit("dve", lambda e, init=init: e.tensor_tensor_scan(out=Hb, data0=Ab, data1=Sb, initial=init,
                                                                             op0=ALU.mult, op1=ALU.add),
                           reads=[B_A, B_S, B_carry], writes=[B_H])
                    if half == 0:
                        S.emit("dve", CP(carry, Hb[:, OWN - 1:OWN]), reads=[B_H], writes=[B_carry])
                    else:
                        bg = proj_job(Gs, BGs, j, lambda k, ch: hT[:, k, OWN + ch * 512:OWN + (ch + 1) * 512], NK, 2,
                                      [B_hT[1]])
                        S.emit("act", ACT(Gb, PS(bg, 2), AF.Silu), reads=BB(bg, 2), writes=[B_G])
                        S.emit("dve", TT(hgT[:, c, :], Hb, Gb, ALU.mult), reads=[B_H, B_G], writes=[B_hg])
                if c % 2 == 1:
                    release()

            if stop < 2:
                raise _Stop()
            o = 0
            qT = carve(o, [128, 2, OWN], BF16); o += 4096
            kT = carve(o, [128, 2, TOK], BF16); o += 8192
            vv = wgT[:].rearrange("p a n d -> p (a n d)").rearrange("p (v w) -> p v w", w=PW)
            pT = carve(o, [128, 1024], BF16); o += 2048
            Ua = carve(o, [128, 2, OWN], F32); o += 8192
            La = carve(o, [128, 2, OWN], F32); o += 8192
            rt1 = carve(o, [128, 512], F32); o += 2048
            rt2 = carve(o, [128, 512], F32); o += 2048
            Gt = carve(o - 4096, [128, OWN], F32)
            B_q, B_k, B_v, B_p, B_U, B_L, B_rt = (Buf(n) for n in ("q", "k", "v", "p", "U", "L", "rt"))
            ph1 = [B_wg, B_xT, B_xcf, B_xcb, B_A, B_I, B_S, B_H, B_G]
            first2 = {"q": True, "k": True, "v": True, "p": True, "U": True, "L": True, "rt": True}

            def W(name, buf):
                if first2[name]:
                    first2[name] = False
                    return [buf] + ph1
                return [buf]

            def rope_apply(raw_fn, c0, T, bufobj, bname):
                for a in range(c0, c0 + T, 512):
                    w = min(512, c0 + T - a)
                    b = alloc(1)
                    S.emit("pe", MM(ps[:, b, 0:w], pm, raw_fn(a, a + w), True, True), reads=[bufobj, B_const],
                           writes=BB(b))
                    S.emit("dve", TT(rt1[0:32, 0:w], ps[0:32, b, 0:w], ropeS[:, a:a + w], ALU.mult),
                           reads=BB(b) + [B_const], writes=W("rt", B_rt))
                    S.emit("dve", TT(rt2[0:32, 0:w], raw_fn(a, a + w)[0:32], ropeC[:, a:a + w], ALU.mult),
                           reads=[bufobj, B_const], writes=[B_rt])
                    S.emit("dve", TT(raw_fn(a, a + w)[0:32], rt1[0:32, 0:w], rt2[0:32, 0:w], ALU.add),
                           reads=[B_rt], writes=[bufobj])

            def attn_unit(jh, g, qsets, mask_ap, nq, dstU, dstL, psview):
                nkb = len(qsets[0][1])
                nst = len(qsets) * nkb * nq
                nb = (nst + 511) // 512
                b = alloc(nb)
                fns = []
                for i in range(nb):
                    w = min(512, nst - i * 512)
                    fns.append(MM(ps[:, b + i, 0:w], ident, mask_ap[:, i * 512:i * 512 + w], True, False, True))
                col = 0
                for qi, (qc, kbs) in enumerate(qsets):
                    for (kc, vi) in kbs:
                        bi_, off = divmod(col, 512)
                        fns.append(MM(ps[:, b + bi_, off:off + nq], kT[:, jh, kc], qT[:, jh, slice(qc.start - OWN, qc.stop - OWN, qc.step)],
                                      False, False, True))
                        col += nq
                S.emit("pe", fns, reads=[B_q, B_k, B_const], writes=BB(b, nb))
                for i in range(nb):
                    w = min(512, nst - i * 512)
                    S.emit("act", ACT(pT[:, i * 512:i * 512 + w], ps[:, b + i, 0:w], AF.Exp, scale=SCALE),
                           reads=BB(b + i), writes=W("p", B_p))
                bu = alloc(1)
                bl = alloc(1)
                fu, fl = [], []
                col = 0
                for qi, (qc, kbs) in enumerate(qsets):
                    for ki, (kc, vi) in enumerate(kbs):
                        st, sp_ = ki == 0, ki == len(kbs) - 1
                        fu.append(MM(ps[:, bu, qi * nq:(qi + 1) * nq], vv[:, vi, jh * 128:(jh + 1) * 128],
                                     pT[:, col:col + nq], st, sp_, True))
                        fl.append(MM(ps[:, bl, qi * nq:(qi + 1) * nq], ones, pT[:, col:col + nq], st, sp_, True))
                        col += nq
                S.emit("pe", fu, reads=[B_v, B_p], writes=BB(bu))
                S.emit("pe", fl, reads=[B_p, B_const], writes=BB(bl))
                if g == 0:
                    S.emit("dve", CP(dstU, psview(bu)), reads=BB(bu), writes=W("U", B_U))
                    S.emit("dve", CP(dstL, psview(bl)), reads=BB(bl), writes=W("L", B_L))
                else:
                    S.emit("dve", TT(dstU, psview(bu), dstU, ALU.add), reads=BB(bu) + [B_U], writes=[B_U])
                    S.emit("dve", TT(dstL, psview(bl), dstL, ALU.add), reads=BB(bl) + [B_L], writes=[B_L])

            for hp in range(4):
                for g in range(3):
                    d = DIL[g]
                    Hg = HIST[g]
                    k0 = OWN - Hg
                    Qs, BQs = next_panel()
                    Ks, BKs = next_panel()
                    Vs, BVs = next_panel()
                    for jh in range(2):
                        b = proj_job(Qs, BQs, jh, lambda k, ch: hT[:, k, OWN + ch * 512:OWN + (ch + 1) * 512], NK, 2,
                                     [B_hT[1]])
                        S.emit("act", ACT(qT[:, jh, :], PS(b, 2), AF.Copy), reads=BB(b, 2), writes=W("q", B_q))
                        rope_apply(lambda a, b_, jh=jh: qT[:, jh, a - OWN:b_ - OWN], OWN, OWN, B_q, "q")
                    pieces = []
                    a = k0
                    while a < TOK:
                        w = min(512, TOK - a) if a >= OWN else min(512, OWN - a)
                        pieces.append((a, w))
                        a += w
                    for jh in range(2):
                        for pi in range(0, len(pieces), 2):
                            grp_p = pieces[pi:pi + 2]
                            b = alloc(len(grp_p))
                            fns = []
                            for k in range(NK):
                                for ch, (a, w) in enumerate(grp_p):
                                    fns.append(MM(ps[:, b + ch, 0:w], Ks[:, k, jh * 128:(jh + 1) * 128], hT[:, k, a:a + w],
                                                  k == 0, k == NK - 1))
                            S.emit("pe", fns, reads=[BKs, B_hT[0], B_hT[1]], writes=BB(b, len(grp_p)))
                            for ch, (a, w) in enumerate(grp_p):
                                S.emit("act", ACT(kT[:, jh, a:a + w], ps[:, b + ch, 0:w], AF.Copy), reads=BB(b + ch),
                                       writes=W("k", B_k))
                        rope_apply(lambda a, b_, jh=jh: kT[:, jh, a:b_], k0, TOK - k0, B_k, "k")
                    vtiles = []
                    for r in range(d):
                        for jb in range(k0 // (128 * d), TOK // (128 * d)):
                            vtiles.append(sl(r + d * 128 * jb, 128, d))
                    nvb = TOK // (128 * d) - k0 // (128 * d)
                    for vi in range(0, len(vtiles), 2):
                        pair = vtiles[vi:vi + 2]
                        b = alloc(1)
                        fns = []
                        for pi, cs in enumerate(pair):
                            for k in range(NK):
                                fns.append(MM(ps[:, b, pi * PW:(pi + 1) * PW], hT[:, k, cs], Vs[:, k, :],
                                              k == 0, k == NK - 1, True))
                        S.emit("pe", fns, reads=[BVs, B_hT[0], B_hT[1]], writes=BB(b))
                        n2 = len(pair) * PW
                        ev[0] += 1
                        dst = vv[:, vi:vi + len(pair), :].rearrange("p a b -> p (a b)")
                        if ev[0] % 2:
                            S.emit("act", ACT(dst, ps[:, b, 0:n2], AF.Copy), reads=BB(b), writes=W("v", B_v))
                        else:
                            S.emit("dve", CP(dst, ps[:, b, 0:n2]), reads=BB(b), writes=W("v", B_v))
                    for jh in range(2):
                        Uh, Lh = Ua[:, jh, :], La[:, jh, :]
                        for u in range(2):
                            if g == 0:
                                qsets = []
                                for n in range(4 * u, 4 * u + 4):
                                    jb = 8 + n
                                    qsets.append((slice(128 * jb, 128 * (jb + 1), 1),
                                                  [(slice(128 * (jb - 1), 128 * jb, 1), jb - 1 - 7),
                                                   (slice(128 * jb, 128 * (jb + 1), 1), jb - 7)]))
                                mk = masks[:, M_A0:M_A0 + 1024] if u == 0 else masks[:, M_A1:M_A1 + 1024]
                                dU, dL = Uh[:, u * 512:(u + 1) * 512], Lh[:, u * 512:(u + 1) * 512]
                                pv = lambda bk: ps[:, bk, :]
                                nq = 128
                            elif g == 1:
                                qsets = []
                                for r in (2 * u, 2 * u + 1):
                                    for jb in (2, 3):
                                        cs = lambda jj, r=r: sl(r + 512 * jj, 128, 4)
                                        qsets.append((cs(jb), [(cs(jb - 1), r * nvb + jb - 2), (cs(jb), r * nvb + jb - 1)]))
                                mk = masks[:, M_G1:M_G1 + 1024]
                                rr = lambda X: X.rearrange("p (l r) -> p r l", r=4)[:, 2 * u:2 * u + 2, :]
                                dU, dL = rr(Uh), rr(Lh)
                                pv = lambda bk: ps[:, bk, :].rearrange("p (r l) -> p r l", r=2)
                                nq = 128
                            else:
                                qsets = []
                                for r in range(8 * u, 8 * u + 8):
                                    qsets.append((sl(r + 16 * 64, 64, 16), [(sl(r, 128, 16), r)]))
                                mk = masks[:, M_G2:M_G2 + 512]
                                rr = lambda X: X.rearrange("p (l r) -> p r l", r=16)[:, 8 * u:8 * u + 8, :]
                                dU, dL = rr(Uh), rr(Lh)
                                pv = lambda bk: ps[:, bk, :].rearrange("p (r l) -> p r l", r=8)
                                nq = 64
                            attn_unit(jh, g, qsets, mk, nq, dU, dL, pv)
                    release()
                AGs, BAGs = next_panel()
                for jh in range(2):
                    b = proj_job(AGs, BAGs, jh, lambda k, ch: hT[:, k, OWN + ch * 512:OWN + (ch + 1) * 512], NK, 2,
                                 [B_hT[1]])
                    S.emit("act", ACT(Gt, PS(b, 2), AF.Silu), reads=BB(b, 2), writes=[B_rt])
                    S.emit("dve", lambda e, jh=jh: e.reciprocal(out=La[:, jh, :], in_=La[:, jh, :]), reads=[B_L],
                           writes=[B_L])
                    S.emit("dve", TT(Ua[:, jh, :], Ua[:, jh, :], La[:, jh, :], ALU.mult), reads=[B_U, B_L], writes=[B_U])
                    S.emit("dve", TT(ogT[:, 2 * hp + jh, :], Ua[:, jh, :], Gt, ALU.mult), reads=[B_U, B_rt],
                           writes=[B_og])
                release()

            if stop < 3:
                raise _Stop()
            o = 0
            sg = [carve(o + i * 4096, [128, OWN], F32) for i in range(4)]
            B_sg = [Buf("sg%d" % i) for i in range(4)]
            ph2 = [B_q, B_k, B_v, B_p, B_U, B_L, B_rt]
            first3 = [True] * 4
            B_mg = B_hT[0]

            def W3(i):
                if first3[i]:
                    first3[i] = False
                    return [B_sg[i]] + ph2
                return [B_sg[i]]

            own_rhs = lambda k, ch: hT[:, k, OWN + ch * 512:OWN + (ch + 1) * 512]
            for mp in range(8):
                GRs, BGRs = next_panel()
                WRs, BWRs = next_panel()
                GAs, BGAs = next_panel()
                WAs, BWAs = next_panel()
                for j in range(2):
                    m = 2 * mp + j
                    s0, s1 = (0, 1) if m % 2 == 0 else (2, 3)
                    b = proj_job(GRs, BGRs, j, own_rhs, NK, 2, [B_hT[1]])
                    S.emit("act", ACT(sg[s0], PS(b, 2), AF.Sigmoid), reads=BB(b, 2), writes=W3(s0))
                    b = proj_job(WRs, BWRs, j, lambda k, ch: hgT[:, k, ch * 512:(ch + 1) * 512], NK, 2, [B_hg])
                    for ch in range(2):
                        S.emit("dve", TT(sg[s0][:, ch * 512:(ch + 1) * 512], ps[:, b + ch, :],
                                         sg[s0][:, ch * 512:(ch + 1) * 512], ALU.mult),
                               reads=BB(b + ch) + [B_sg[s0]], writes=[B_sg[s0]])
                    b = proj_job(GAs, BGAs, j, own_rhs, NK, 2, [B_hT[1]])
                    S.emit("act", ACT(sg[s1], PS(b, 2), AF.Sigmoid), reads=BB(b, 2), writes=W3(s1))
                    b = proj_job(WAs, BWAs, j, lambda k, ch: ogT[:, k, ch * 512:(ch + 1) * 512], 8, 2, [B_og])
                    for ch in range(2):
                        S.emit("dve", TT(sg[s1][:, ch * 512:(ch + 1) * 512], ps[:, b + ch, :],
                                         sg[s1][:, ch * 512:(ch + 1) * 512], ALU.mult),
                               reads=BB(b + ch) + [B_sg[s1]], writes=[B_sg[s1]])
                    S.emit("dve", TT(hT[:, m, 0:OWN], sg[s0], sg[s1], ALU.add), reads=[B_sg[s0], B_sg[s1]],
                           writes=[B_mg])
                release()

            if stop < 4:
                raise _Stop()
            hg_flat = hgT[:].rearrange("p a b -> p (a b)")
            wo_ap, wo_buf = [], []
            for p in range(8):
                if p < 4:
                    apv = hg_flat[:, p * NK * PW:(p + 1) * NK * PW].rearrange("p (k n) -> p k n", n=PW)
                    bf = B_hg
                    S.emit("pool", DMA(apv, w_o[:, p * PW:(p + 1) * PW].rearrange("(k p) n -> p k n", p=128)),
                           writes=[bf], dma="d_wo")
                else:
                    s = p - 4
                    apv = slot_ap(s)
                    bf = B_slot[s]
                    S.emit("pool", DMA(apv, w_o[:, p * PW:(p + 1) * PW].rearrange("(k p) n -> p k n", p=128)),
                           writes=[bf], dma="d_w%d" % s)
                wo_ap.append(apv)
                wo_buf.append(bf)
            o = 0
            gpost = carve(o, [128, D], F32); o += 8192
            xo = [carve(o, [128, D], F32)] * 2; o += 8192
            yn = [carve(o + i * 8192, [128, D], F32) for i in range(2)]; o += 16384
            B_gp, B_yn = Buf("gpost"), [Buf("yn0"), Buf("yn1")]
            _bxo = Buf("xo0")
            B_xo = [_bxo, _bxo]
            S.emit("sp", DMA(gpost, gpost_d), writes=[B_gp] + B_sg, dma="d_gp")
            first4 = {"xo0": True, "xo1": True, "yn0": True, "yn1": True}

            def W4(bf):
                if first4[bf.name]:
                    first4[bf.name] = False
                    return [bf] + B_sg
                return [bf]

            ss2 = stat[:, 0:8]
            rs2 = stat[:, 16:24]
            for t in range(8):
                i = t % 2
                S.emit("sp", DMA(xo[i], x_in[OWN + t * 128:OWN + (t + 1) * 128, :]), writes=W4(B_xo[i]),
                       dma="d_x0")
                b = alloc(4)
                fns = []
                for k in range(NK):
                    for p in range(8):
                        fns.append(MM(ps[:, b + p // 2, (p % 2) * PW:(p % 2 + 1) * PW], hT[:, k, t * 128:(t + 1) * 128],
                                      wo_ap[p][:, k, :], k == 0 and p % 2 == 0, k == NK - 1, True))
                S.emit("pe", fns, reads=[B_mg] + list({id(x): x for x in wo_buf}.values()), writes=BB(b, 4))
                S.emit("act", ACT(yn[i][:, 0:1024], PS(b, 2), AF.Square, accum=ss2[:, t:t + 1]), reads=BB(b, 4),
                       writes=W4(B_yn[i]) + [B_stat])
                S.emit("act", ACT(yn[i][:, 1024:2048], PS(b + 2, 2), AF.Square, accum=stat[:, 8 + t:9 + t]), reads=BB(b, 4),
                       writes=[B_yn[i], B_stat])
                S.emit("dve", TT(ss2[:, t:t + 1], ss2[:, t:t + 1], stat[:, 8 + t:9 + t], ALU.add), reads=[B_stat],
                       writes=[B_stat])
                S.emit("act", ACT(rs2[:, t:t + 1], ss2[:, t:t + 1], AF.Sqrt, bias=EPS, scale=1.0 / D), reads=[B_stat],
                       writes=[B_stat])
                S.emit("dve", lambda e, t=t: e.reciprocal(out=rs2[:, t:t + 1], in_=rs2[:, t:t + 1]), reads=[B_stat],
                       writes=[B_stat])
                S.emit("act", ACT(yn[i][:, 0:1024], PS(b, 2), AF.Copy, scale=rs2[:, t:t + 1]), reads=BB(b, 4) + [B_stat],
                       writes=[B_yn[i]])
                S.emit("act", ACT(yn[i][:, 1024:2048], PS(b + 2, 2), AF.Copy, scale=rs2[:, t:t + 1]),
                       reads=BB(b, 4) + [B_stat], writes=[B_yn[i]])
                S.emit("dve", TT(yn[i], yn[i], gpost, ALU.mult), reads=[B_yn[i], B_gp], writes=[B_yn[i]])
                S.emit("dve", TT(yn[i], yn[i], xo[i], ALU.add), reads=[B_yn[i], B_xo[i]], writes=[B_yn[i]])
                S.emit("sp", DMA(y_out[t * 128:(t + 1) * 128, :], yn[i]), reads=[B_yn[i]], dma="d_y%d" % i)


        try:
          _build_phases()
        except _Stop:
          pass
        if stop < 4:
            S.emit("sp", DMA(y_out[0:128, :], tmp[:, 0:D]), dma="d_y0", force=True)
        if debug:
            B_dbg = Buf("dbg")
            S.emit("sp", DMA(dbg["hT"], hT[:].rearrange("p a b -> p (a b)")), reads=B_hT, dma="d_dbg")
            S.emit("sp", DMA(dbg["ogT"], ogT[:].rearrange("p a b -> p (a b)")), reads=[B_og], dma="d_dbg")
            S.emit("sp", DMA(dbg["hgT"], hgT[:].rearrange("p a b -> p (a b)")), reads=[B_hg], dma="d_dbg")

        if os.environ.get("KLOG"):
            for l in S.log:
                print("OP", l)
        final_waits = [(k, v) for k, v in S.cnt.items() if k.startswith("d_y") or k == "d_dbg"]

        sems = {n: es.enter_context(nc.semaphore(n)) for n in sorted(S.semnames)}
        with nc.Block() as block:
            @block.tensor
            def _(e):
                S.replay("pe", e, sems)

            @block.scalar
            def _(e):
                S.replay("act", e, sems)

            @block.vector
            def _(e):
                S.replay("dve", e, sems)

            @block.gpsimd
            def _(e):
                S.replay("pool", e, sems)

            @block.sync
            def _(e):
                S.replay("sp", e, sems)
                for k, v in final_waits:
                    e.wait_ge(sems[k], v)
    return nc


def _host_inputs(inputs):
    x = np.asarray(inputs["x"], dtype=np.float32)
    f = lambda k: np.ascontiguousarray(np.asarray(inputs[k], dtype=np.float32)[0])
    w_in, w_ro, w_ao, w_o = f("w_in"), f("w_rnn_out"), f("w_attn_out"), f("w_o")
    w_rg, w_ig = f("w_rg"), f("w_ig")
    colT = lambda v: np.ascontiguousarray(v.reshape(16, 128).T)
    conv_w = f("conv_w")
    cw = np.ascontiguousarray(conv_w.reshape(4, 16, 128).transpose(2, 1, 0).reshape(128, 64))
    gpost = np.ascontiguousarray(np.broadcast_to(f("ln_post_g")[None, :], (128, D)))
    bf = ml_dtypes.bfloat16
    cbf = np.zeros((128, 384), dtype=np.float32)
    cbf[:, 0:128] = np.eye(128)
    cbf[:, 128:256] = 1.0
    for i in range(16):
        cbf[i + 16, 256 + i] = -1.0
        cbf[i, 256 + 16 + i] = 1.0
    cbf = cbf.astype(bf)
    k = np.arange(128)[:, None]
    q = np.arange(128)[None, :]
    P = np.where(k >= q, 0.0, NEG).astype(np.float32)
    C = np.where(k <= q, 0.0, NEG).astype(np.float32)
    inv = np.exp(-math.log(500000.0) * np.arange(16, dtype=np.float32) * (2.0 / 32)).astype(np.float32)
    maps = []
    for core in range(8):
        b, half = divmod(core, 2)
        if half == 1:
            xin = np.ascontiguousarray(x[b])
        else:
            xin = np.concatenate([np.zeros((OWN, D), np.float32), x[b, :OWN]], axis=0)
        vecs = np.zeros((128, V_N), np.float32)
        vecs[:, V_GPRE:V_GPRE + 16] = colT(f("ln_pre_g"))
        vecs[:, V_CW:V_CW + 64] = cw
        vecs[:, V_CB:V_CB + 16] = colT(f("conv_b"))
        vecs[:, V_BRG:V_BRG + 16] = colT(f("b_rg"))
        vecs[:, V_BIG:V_BIG + 16] = colT(f("b_ig"))
        vecs[:, V_LAM:V_LAM + 16] = colT(f("lru_lambda"))
        vecs[:, V_HM] = float(half)
        pos = (np.arange(TOK) + (half - 1) * OWN).astype(np.float32)
        ang = pos[None, :] * np.concatenate([inv, inv])[:, None]
        rope = np.concatenate([np.cos(ang), np.sin(ang)], axis=1).astype(np.float32).astype(bf)
        Ph = P if half == 1 else np.full((128, 128), NEG, np.float32)
        mk = np.zeros((128, M_N), np.float32)
        mk[:, M_A0:M_A0 + 1280] = np.concatenate([Ph, C, P, C, P, C, P, C, P, C], axis=1)
        mk[:, M_G1:M_G1 + 1024] = np.concatenate([Ph, C, P, C, Ph, C, P, C], axis=1)
        q64 = np.arange(64)[None, :]
        C2 = np.where((k <= q64 + 64) & ((half == 1) | (k >= 64)), 0.0, NEG).astype(np.float32)
        mk[:, M_G2:M_G2 + 512] = np.tile(C2, (1, 8))
        maps.append({"x_in": xin, "w_in": w_in, "w_ro": w_ro, "w_ao": w_ao, "w_o": w_o, "w_rg": w_rg, "w_ig": w_ig,
                     "vecs": vecs, "gpost": gpost, "cbf": cbf, "rope": np.ascontiguousarray(rope),
                     "masks": mk.astype(bf)})
    return maps


def kernel(**inputs):
    maps = _host_inputs(inputs)
    nc = build_nc(debug=False)
    res = run_bass_kernel_spmd(nc, maps, core_ids=list(range(8)))
    out = np.zeros((4, 2048, D), dtype=np.float32)
    for core in range(8):
        b, half = divmod(core, 2)
        out[b, half * OWN:(half + 1) * OWN] = res.results[core]["y"]
    return out
```

```python
import contextlib
import math
import os

import numpy as np
import ml_dtypes

import concourse.bass as bass
import concourse.mybir as mybir
from concourse.bass_utils import run_bass_kernel_spmd

F32 = mybir.dt.float32
BF16 = mybir.dt.bfloat16
AF = mybir.ActivationFunctionType
ALU = mybir.AluOpType

D = 2048
NK = 16
TOK = 2048
OWN = 1024
DIN = 18432
NEG = -30000.0
EPS = 1e-6
DIL = (1, 4, 16)
HIST = (128, 512, 1024)
SCALE = 1.0 / math.sqrt(128.0)
PW = 256
NSLOT = 4

V_GPRE, V_CW, V_CB, V_BRG, V_BIG, V_LAM, V_HM, V_N = 0, 16, 80, 96, 112, 128, 144, 145
M_A0, M_A1, M_G1, M_G2, M_N = 0, 256, 1280, 2304, 2816

ENGS = ("pe", "act", "dve", "pool", "sp")


class Buf:
    def __init__(self, name):
        self.name = name
        self.w = {}
        self.r = {}


class Sched:
    def __init__(self):
        self.ops = {e: [] for e in ENGS}
        self.cnt = {}
        self.waited = {}
        self.semnames = set("p_" + e for e in ENGS)
        self.limit = int(os.environ.get("KLIM", "0")) or None
        self.n = 0
        self.log = []

    def _need(self, eng, deps, sem, count, is_dma):
        if self.waited.get((eng, sem), 0) >= count:
            return
        deps[sem] = max(deps.get(sem, 0), count)

    def emit(self, eng, fns, reads=(), writes=(), dma=None, force=False):
        self.n += 1
        if self.limit is not None and self.n > self.limit and not force:
            return ("none", 0)
        self.log.append((self.n, eng, [b.name for b in reads], [b.name for b in writes]))
        deps = {}
        isd = dma is not None
        for b in reads:
            for k, v in b.w.items():
                self._need(eng, deps, k, v, isd)
        for b in writes:
            for k, v in b.w.items():
                self._need(eng, deps, k, v, isd)
            for k, v in b.r.items():
                self._need(eng, deps, k, v, isd)
        for k, v in deps.items():
            self.waited[(eng, k)] = v
        if isd:
            sem, inc = dma, 16
            self.semnames.add(dma)
        else:
            sem, inc = "p_" + eng, 1
        self.cnt[sem] = self.cnt.get(sem, 0) + inc
        c = self.cnt[sem]
        for b in writes:
            b.w = {sem: c}
            b.r = {}
        for b in reads:
            b.r[sem] = max(b.r.get(sem, 0), c)
        if not isinstance(fns, list):
            fns = [fns]
        self.ops[eng].append((sorted(deps.items()), fns, sem, inc))
        return (sem, c)

    def replay(self, eng, e, sems):
        for deps, fns, sem, inc in self.ops[eng]:
            for k, v in deps:
                e.wait_ge(sems[k], v)
            for f in fns[:-1]:
                f(e)
            fns[-1](e).then_inc(sems[sem], inc)


def MM(out, lhsT, rhs, start, stop, skip=False):
    if skip:
        return lambda e: e.matmul(out, lhsT, rhs, start=start, stop=stop, skip_group_check=True)
    return lambda e: e.matmul(out, lhsT, rhs, start=start, stop=stop)


def ACT(out, in_, func, bias=None, scale=None, accum=None):
    kw = {}
    if bias is not None:
        kw["bias"] = bias
    if scale is not None:
        kw["scale"] = scale
    if accum is not None:
        kw["accum_out"] = accum
    return lambda e: e.activation(out=out, in_=in_, func=func, **kw)


def TT(out, a, b, op):
    return lambda e: e.tensor_tensor(out=out, in0=a, in1=b, op=op)


def TS(out, a, s1, op0, s2=None, op1=None):
    if op1 is None:
        return lambda e: e.tensor_scalar(out=out, in0=a, scalar1=s1, scalar2=None, op0=op0)
    return lambda e: e.tensor_scalar(out=out, in0=a, scalar1=s1, scalar2=s2, op0=op0, op1=op1)


def CP(out, a):
    return lambda e: e.tensor_copy(out=out, in_=a)


def sl(start, n, step):
    return slice(start, start + (n - 1) * step + 1, step)


def DMA(out, in_):
    return lambda e: e.dma_start(out=out, in_=in_)


class _Stop(Exception):
    pass


def build_nc(debug=False, stop=9):
    nc = bass.Bass("TRN2", target_bir_lowering=False)
    dt = lambda n, s, d, k="ExternalInput": nc.dram_tensor(n, s, d, kind=k).ap()
    x_in = dt("x_in", [TOK, D], F32)
    w_in = dt("w_in", [D, DIN], F32)
    w_ro = dt("w_ro", [D, D], F32)
    w_ao = dt("w_ao", [1024, D], F32)
    w_o = dt("w_o", [D, D], F32)
    w_rg = dt("w_rg", [16, 128, 128], F32)
    w_ig = dt("w_ig", [16, 128, 128], F32)
    vecs_d = dt("vecs", [128, V_N], F32)
    gpost_d = dt("gpost", [128, D], F32)
    cbf_d = dt("cbf", [128, 384], BF16)
    rope_d = dt("rope", [32, 2 * TOK], BF16)
    mask_d = dt("masks", [128, M_N], BF16)
    y_out = dt("y", [OWN, D], F32, "ExternalOutput")
    dbg = {}
    if debug:
        dbg["hT"] = dt("d_hT", [128, NK * TOK], BF16, "ExternalOutput")
        dbg["hgT"] = dt("d_hgT", [128, NK * OWN], BF16, "ExternalOutput")
        dbg["ogT"] = dt("d_ogT", [128, 8 * OWN], BF16, "ExternalOutput")

    S = Sched()
    TEMPB = 37 * 1024
    with contextlib.ExitStack() as es:
        sb = lambda n, s, d: es.enter_context(nc.sbuf_tensor("s_" + n, s, d))
        hT = sb("hT", [128, NK, TOK], BF16)
        hgT = sb("hgT", [128, NK, OWN], BF16)
        ogT = sb("ogT", [128, 8, OWN], BF16)
        ring = sb("ring", [128, NSLOT, NK * PW], BF16)
        cbf = sb("cbf", [128, 384], BF16)
        rope = sb("rope", [32, 2 * TOK], BF16)
        masks = sb("masks", [128, M_N], BF16)
        vecs = sb("vecs", [128, V_N], F32)
        stat = sb("stat", [128, 64], F32)
        diag = sb("diag", [128, 4, 128], BF16)
        tmp = sb("tmp", [128, TEMPB // 4], F32)
        wgT = sb("wgT", [128, 2, 16, 128], BF16)
        ps = es.enter_context(nc.psum_tensor("ps", [128, 8, 512], F32))

        ident = cbf[:, 0:128]
        ones = cbf[:, 128:256]
        pm = cbf[:, 256:384]
        ropeC = rope[:, 0:TOK]
        ropeS = rope[:, TOK:2 * TOK]

        def carve(off, shape, d):
            esz = 2 if d == BF16 else 4
            n = int(np.prod(shape[1:]))
            nb = n * esz
            assert off % 4 == 0 and nb % 4 == 0 and off + nb <= TEMPB, (off, nb)
            v = tmp[:, off // 4:(off + nb) // 4]
            if d != F32:
                v = v.bitcast(d)
            if len(shape) == 3:
                v = v.rearrange("p (a b) -> p a b", a=shape[1])
            return v

        B_hT = [Buf("hT_hist"), Buf("hT_own")]
        B_hg, B_og = Buf("hgT"), Buf("ogT")
        B_const = Buf("const")
        B_stat = Buf("stat")
        B_bank = [Buf("bank%d" % i) for i in range(8)]
        B_slot = [Buf("slot%d" % i) for i in range(NSLOT)]
        bank_ptr = [0]

        def alloc(n):
            p = bank_ptr[0]
            if n > 1 and p % n:
                p += n - p % n
            if p + n > 8:
                p = 0
            bank_ptr[0] = (p + n) % 8
            return p

        def PS(b, n=1, w=None):
            if n == 1:
                return ps[:, b, :] if w is None else ps[:, b, 0:w]
            return ps[:, b:b + n, :].rearrange("p a b -> p (a b)")

        def BB(b, n=1):
            return B_bank[b:b + n]

        S.emit("sp", DMA(vecs[:], vecs_d), writes=[B_const], dma="d_c")
        S.emit("sp", DMA(cbf[:], cbf_d), writes=[B_const], dma="d_c")
        S.emit("sp", DMA(rope[:], rope_d), writes=[B_const], dma="d_c")
        S.emit("sp", DMA(masks[:], mask_d), writes=[B_const], dma="d_c")
        cch = stat[:, 48:64]
        S.emit("act", ACT(stat[:, 32:48], vecs[:, V_LAM:V_LAM + 16], AF.Exp, scale=-1.0),
               reads=[B_const], writes=[B_stat])
        S.emit("act", ACT(stat[:, 32:48], stat[:, 32:48], AF.Ln, bias=1.0), reads=[B_stat], writes=[B_stat])
        S.emit("dve", TS(cch, stat[:, 32:48], -8.0, ALU.mult), reads=[B_stat], writes=[B_stat])

        panels = []

        def wcols(w, c0, nk=NK):
            return (w[0:nk * 128, c0:c0 + PW], nk)

        for cp in range(8):
            panels.append(wcols(w_in, cp * PW))
            panels.append(wcols(w_in, 2048 + cp * PW))
        for hp in range(4):
            for g in range(3):
                for base in (4096, 7168, 10240):
                    panels.append(wcols(w_in, base + g * 1024 + hp * PW))
            panels.append(wcols(w_in, 13312 + hp * PW))
        for mp in range(8):
            panels.append(wcols(w_in, 14336 + mp * PW))
            panels.append(wcols(w_ro, mp * PW))
            panels.append(wcols(w_in, 16384 + mp * PW))
            panels.append(wcols(w_ao, mp * PW, 8))
        n_ring_panels = len(panels)
        pstate = {"issued": 0, "next": 0}

        def slot_ap(i):
            return ring[:, i, :].rearrange("p (k n) -> p k n", n=PW)

        def issue_upto(j):
            while pstate["issued"] <= min(j, n_ring_panels - 1):
                i = pstate["issued"]
                src, nk = panels[i]
                s = i % NSLOT
                S.emit("pool", DMA(slot_ap(s)[:, 0:nk, :], src.rearrange("(k p) n -> p k n", p=128)),
                       writes=[B_slot[s]], dma="d_w%d" % s)
                pstate["issued"] += 1

        def next_panel():
            i = pstate["next"]
            pstate["next"] += 1
            issue_upto(i)
            s = i % NSLOT
            return slot_ap(s), B_slot[s]

        def release(keep=0):
            issue_upto(pstate["next"] + NSLOT - 1 - keep)

        if not os.environ.get("KSKIP_W"):
            release()

        def _build_phases():
            if stop < 0:
                raise _Stop()
            xt = [carve(i * 8192, [128, D], F32) for i in range(2)]
            xs = [carve(16384 + i * 4096, [128, D], BF16) for i in range(4)]
            B_xt = [Buf("xt0"), Buf("xt1")]
            B_xs = [Buf("xs%d" % i) for i in range(4)]
            ev = [0]

            def evac_scaled(out, in_, col, reads, writes):
                ev[0] += 1
                if ev[0] % 2:
                    S.emit("act", ACT(out, in_, AF.Copy, scale=col), reads=reads, writes=writes)
                else:
                    S.emit("dve", TS(out, in_, col, ALU.mult), reads=reads, writes=writes)

            for grp in range(4):
                for j in range(4):
                    t = grp * 4 + j
                    S.emit("sp", DMA(xt[t % 2], x_in[t * 128:(t + 1) * 128, :]), writes=[B_xt[t % 2]],
                           dma="d_x%d" % (t % 2))
                    S.emit("act", ACT(xs[j], xt[t % 2], AF.Square, accum=stat[:, t:t + 1]),
                           reads=[B_xt[t % 2]], writes=[B_xs[j], B_stat])
                    S.emit("act", ACT(stat[:, 16 + t:17 + t], stat[:, t:t + 1], AF.Sqrt, bias=EPS, scale=1.0 / D),
                           reads=[B_stat], writes=[B_stat])
                    S.emit("dve", lambda e, t=t: e.reciprocal(out=stat[:, 16 + t:17 + t], in_=stat[:, 16 + t:17 + t]),
                           reads=[B_stat], writes=[B_stat])
                    S.emit("act", ACT(xs[j], xt[t % 2], AF.Copy, scale=stat[:, 16 + t:17 + t]),
                           reads=[B_xt[t % 2], B_stat], writes=[B_xs[j]])
                for kc in range(NK):
                    b = alloc(1)
                    S.emit("pe", [MM(ps[:, b, j * 128:(j + 1) * 128], xs[j][:, kc * 128:(kc + 1) * 128], ident,
                                     True, True) for j in range(4)],
                           reads=B_xs + [B_const], writes=BB(b))
                    evac_scaled(hT[:, kc, grp * 512:(grp + 1) * 512], PS(b), vecs[:, V_GPRE + kc:V_GPRE + kc + 1],
                                BB(b) + [B_const], [B_hT[grp // 2]])

            def proj_job(slot, bslot, j, rhs_fn, nk, nch, extra_reads):
                b = alloc(nch)
                fns = []
                for k in range(nk):
                    for ch in range(nch):
                        fns.append(MM(ps[:, b + ch, :], slot[:, k, j * 128:(j + 1) * 128], rhs_fn(k, ch),
                                      k == 0, k == nk - 1))
                S.emit("pe", fns, reads=[bslot] + extra_reads, writes=BB(b, nch))
                return b

            if stop < 1:
                raise _Stop()
            o = 0
            wrg = wgT[:, 0, :, :]
            wig = wgT[:, 1, :, :]
            xT2 = [carve(o + i * 4112, [128, 2056], BF16) for i in range(2)]; o += 8224
            xcb2 = [carve(o + i * 2048, [128, OWN], BF16) for i in range(2)]; o += 4096
            diag2 = [carve(o + i * 1024, [128, 4, 128], BF16) for i in range(2)]; o += 2048
            Ab = carve(o, [128, OWN], F32); o += 4096
            Ib = carve(o, [128, OWN], F32); o += 4096
            Sb = carve(o, [128, OWN], F32); o += 4096
            Hb = carve(o, [128, OWN], F32); o += 4096
            Gb = carve(o, [128, OWN], F32); o += 4096
            B_wg = Buf("wg")
            B_xT2 = [Buf("xT0"), Buf("xT1")]
            B_xcb2 = [Buf("xcb0"), Buf("xcb1")]
            B_dg2 = [Buf("dg0"), Buf("dg1")]
            B_A, B_I, B_S, B_H, B_G, B_carry = (Buf(n) for n in ("A", "I", "S", "H", "G", "carry"))
            carry = stat[:, 40:41]
            ph0 = B_xt + B_xs
            S.emit("pool", DMA(wrg, w_rg.rearrange("n c d -> c n d")), reads=[], writes=[B_wg] + ph0, dma="d_g")
            S.emit("pool", DMA(wig, w_ig.rearrange("n c d -> c n d")), writes=[B_wg], dma="d_g")
            for i in range(2):
                S.emit("dve", lambda e, i=i: e.memset(xT2[i][:, 0:4], 0.0), writes=[B_xT2[i]] + ph0)
            pairX, pairG = {}, {}
            NC1 = int(os.environ.get('KNC', '16'))

            def stageA(c):
                if c % 2 == 0:
                    pairX[c // 2] = next_panel()
                Xs, BXs = pairX[c // 2]
                j = c % 2
                for tap in range(4):
                    S.emit("dve", TS(diag2[c % 2][:, tap, :], ident,
                                     vecs[:, V_CW + c * 4 + tap:V_CW + c * 4 + tap + 1], ALU.mult),
                           reads=[B_const], writes=[B_dg2[c % 2]])
                for half in range(2):
                    c0 = half * OWN
                    b = proj_job(Xs, BXs, j, lambda k, ch: hT[:, k, c0 + ch * 512:c0 + (ch + 1) * 512], NK, 2,
                                 [B_hT[half]])
                    S.emit("act", ACT(xT2[c % 2][:, 4 + c0:4 + c0 + OWN], PS(b, 2), AF.Copy), reads=BB(b, 2),
                           writes=[B_xT2[c % 2]])

            def stageB(c, half):
                j = c % 2
                c0 = half * OWN
                xTc, dg = xT2[c % 2], diag2[c % 2]
                xcb, B_xcb = xcb2[half], B_xcb2[half]
                if half == 1:
                    if c % 2 == 0:
                        pairG[c // 2] = next_panel()
                    Gs, BGs = pairG[c // 2]
                    bg = proj_job(Gs, BGs, j, lambda k, ch: hT[:, k, OWN + ch * 512:OWN + (ch + 1) * 512], NK, 2,
                                  [B_hT[1]])
                    S.emit("act", ACT(Gb, PS(bg, 2), AF.Silu), reads=BB(bg, 2), writes=[B_G])
                b2 = alloc(2)
                fns = []
                for tap in range(4):
                    for ch in range(2):
                        fns.append(MM(ps[:, b2 + ch, :], dg[:, tap, :],
                                      xTc[:, c0 + ch * 512 + tap + 1:c0 + ch * 512 + tap + 513], tap == 0, tap == 3))
                S.emit("pe", fns, reads=[B_dg2[c % 2], B_xT2[c % 2]], writes=BB(b2, 2))
                cb = vecs[:, V_CB + c:V_CB + c + 1]
                S.emit("act", ACT(xcb, PS(b2, 2), AF.Identity, bias=cb), reads=BB(b2, 2) + [B_const], writes=[B_xcb])
                br = alloc(2)
                S.emit("pe", [MM(ps[:, br + ch, :], wrg[:, c, :], xcb[:, ch * 512:(ch + 1) * 512], True, True)
                              for ch in range(2)], reads=[B_wg, B_xcb], writes=BB(br, 2))
                bi = alloc(2)
                S.emit("pe", [MM(ps[:, bi + ch, :], wig[:, c, :], xcb[:, ch * 512:(ch + 1) * 512], True, True)
                              for ch in range(2)], reads=[B_wg, B_xcb], writes=BB(bi, 2))
                S.emit("act", ACT(Ab, PS(br, 2), AF.Sigmoid, bias=vecs[:, V_BRG + c:V_BRG + c + 1]),
                       reads=BB(br, 2) + [B_const], writes=[B_A])
                S.emit("act", ACT(Ib, PS(bi, 2), AF.Sigmoid, bias=vecs[:, V_BIG + c:V_BIG + c + 1]),
                       reads=BB(bi, 2) + [B_const], writes=[B_I])
                S.emit("act", ACT(Ab, Ab, AF.Exp, scale=cch[:, c:c + 1]), reads=[B_A, B_stat], writes=[B_A])
                S.emit("dve", TT(Sb, Ab, Ab, ALU.mult), reads=[B_A], writes=[B_S])
                S.emit("act", ACT(Sb, Sb, AF.Sqrt, bias=1.0000001, scale=-1.0), reads=[B_S], writes=[B_S])
                if half == 0:
                    S.emit("dve", lambda e: e.scalar_tensor_tensor(out=Ib, in0=Ib, scalar=vecs[:, V_HM:V_HM + 1],
                                                                   in1=xcb, op0=ALU.mult, op1=ALU.mult),
                           reads=[B_I, B_xcb, B_const], writes=[B_I])
                else:
                    S.emit("dve", TT(Ib, Ib, xcb, ALU.mult), reads=[B_I, B_xcb], writes=[B_I])
                S.emit("dve", TT(Sb, Sb, Ib, ALU.mult), reads=[B_S, B_I], writes=[B_S])
                init = 0.0 if half == 0 else carry
                S.emit("dve", lambda e, init=init: e.tensor_tensor_scan(out=Hb, data0=Ab, data1=Sb, initial=init,
                                                                         op0=ALU.mult, op1=ALU.add),
                       reads=[B_A, B_S, B_carry], writes=[B_H])
                if half == 0:
                    S.emit("dve", CP(carry, Hb[:, OWN - 1:OWN]), reads=[B_H], writes=[B_carry])
                else:
                    S.emit("dve", TT(hgT[:, c, :], Hb, Gb, ALU.mult), reads=[B_H, B_G], writes=[B_hg])

            stageA(0)
            for c in range(NC1):
                stageB(c, 0)
                if c + 1 < NC1:
                    stageA(c + 1)
                stageB(c, 1)
                if c % 2 == 1:
                    release(1 if (c + 1 < NC1) else 0)

            if stop < 2:
                raise _Stop()
            o = 0
            qT = carve(o, [128, 2, OWN], BF16); o += 4096
            kT = carve(o, [128, 2, TOK], BF16); o += 8192
            vv = wgT[:].rearrange("p a n d -> p (a n d)").rearrange("p (v w) -> p v w", w=PW)
            pT2 = [carve(o + i * 2048, [128, 1024], BF16) for i in range(2)]; o += 4096
            Ua = carve(o, [128, 2, OWN], F32); o += 8192
            La = carve(o, [128, 2, OWN], F32); o += 8192
            rt1 = carve(o, [128, 512], F32); o += 2048
            rt2 = carve(o, [128, 512], F32); o += 2048
            Gt = carve(o - 4096, [128, OWN], F32)
            B_q, B_k, B_v, B_U, B_L, B_rt = (Buf(n) for n in ("q", "k", "v", "U", "L", "rt"))
            B_p2 = [Buf("p0"), Buf("p1")]
            ph1 = [B_wg] + B_xT2 + B_xcb2 + B_dg2 + [B_A, B_I, B_S, B_H, B_G]
            first2 = {"q": True, "k": True, "v": True, "p0": True, "p1": True, "U": True, "L": True, "rt": True}

            def W(name, buf):
                if first2[name]:
                    first2[name] = False
                    return [buf] + ph1
                return [buf]

            def rope_apply(raw_fn, c0, T, bufobj, bname):
                for a in range(c0, c0 + T, 512):
                    w = min(512, c0 + T - a)
                    b = alloc(1)
                    S.emit("pe", MM(ps[:, b, 0:w], pm, raw_fn(a, a + w), True, True), reads=[bufobj, B_const],
                           writes=BB(b))
                    S.emit("dve", TT(rt1[0:32, 0:w], ps[0:32, b, 0:w], ropeS[:, a:a + w], ALU.mult),
                           reads=BB(b) + [B_const], writes=W("rt", B_rt))
                    S.emit("dve", TT(rt2[0:32, 0:w], raw_fn(a, a + w)[0:32], ropeC[:, a:a + w], ALU.mult),
                           reads=[bufobj, B_const], writes=[B_rt])
                    S.emit("dve", TT(raw_fn(a, a + w)[0:32], rt1[0:32, 0:w], rt2[0:32, 0:w], ALU.add),
                           reads=[B_rt], writes=[bufobj])

            def attn_st(pi, jh, g, qsets, mask_ap, nq, dstU, dstL, psview):
                pT, B_p = pT2[pi], B_p2[pi]
                nkb = len(qsets[0][1])
                nst = len(qsets) * nkb * nq
                nb = (nst + 511) // 512
                b = alloc(nb)
                fns = []
                for i in range(nb):
                    w = min(512, nst - i * 512)
                    fns.append(MM(ps[:, b + i, 0:w], ident, mask_ap[:, i * 512:i * 512 + w], True, False, True))
                col = 0
                for qi, (qc, kbs) in enumerate(qsets):
                    for (kc, vi) in kbs:
                        bi_, off = divmod(col, 512)
                        fns.append(MM(ps[:, b + bi_, off:off + nq], kT[:, jh, kc],
                                      qT[:, jh, slice(qc.start - OWN, qc.stop - OWN, qc.step)], False, False, True))
                        col += nq
                S.emit("pe", fns, reads=[B_q, B_k, B_const], writes=BB(b, nb))
                for i in range(nb):
                    w = min(512, nst - i * 512)
                    S.emit("act", ACT(pT[:, i * 512:i * 512 + w], ps[:, b + i, 0:w], AF.Exp, scale=SCALE),
                           reads=BB(b + i), writes=W("p%d" % pi, B_p))
                return (pi, jh, g, qsets, nq, dstU, dstL, psview)

            def attn_pv(st):
                pi, jh, g, qsets, nq, dstU, dstL, psview = st
                pT, B_p = pT2[pi], B_p2[pi]
                bu = alloc(1)
                bl = alloc(1)
                fu, fl = [], []
                col = 0
                for qi, (qc, kbs) in enumerate(qsets):
                    for ki, (kc, vi) in enumerate(kbs):
                        st_, sp_ = ki == 0, ki == len(kbs) - 1
                        fu.append(MM(ps[:, bu, qi * nq:(qi + 1) * nq], vv[:, vi, jh * 128:(jh + 1) * 128],
                                     pT[:, col:col + nq], st_, sp_, True))
                        fl.append(MM(ps[:, bl, qi * nq:(qi + 1) * nq], ones, pT[:, col:col + nq], st_, sp_, True))
                        col += nq
                S.emit("pe", fu, reads=[B_v, B_p], writes=BB(bu))
                S.emit("pe", fl, reads=[B_p, B_const], writes=BB(bl))
                if g == 0:
                    S.emit("dve", CP(dstU, psview(bu)), reads=BB(bu), writes=W("U", B_U))
                    S.emit("dve", CP(dstL, psview(bl)), reads=BB(bl), writes=W("L", B_L))
                else:
                    S.emit("dve", TT(dstU, psview(bu), dstU, ALU.add), reads=BB(bu) + [B_U], writes=[B_U])
                    S.emit("dve", TT(dstL, psview(bl), dstL, ALU.add), reads=BB(bl) + [B_L], writes=[B_L])

            for hp in range(4):
                for g in range(3):
                    d = DIL[g]
                    Hg = HIST[g]
                    k0 = OWN - Hg
                    Qs, BQs = next_panel()
                    Ks, BKs = next_panel()
                    Vs, BVs = next_panel()
                    ropes = []
                    for jh in range(2):
                        b = proj_job(Qs, BQs, jh, lambda k, ch: hT[:, k, OWN + ch * 512:OWN + (ch + 1) * 512], NK, 2,
                                     [B_hT[1]])
                        S.emit("act", ACT(qT[:, jh, :], PS(b, 2), AF.Copy), reads=BB(b, 2), writes=W("q", B_q))
                        ropes.append((lambda a, b_, jh=jh: qT[:, jh, a - OWN:b_ - OWN], OWN, OWN, B_q, "q"))
                    pieces = []
                    a = k0
                    while a < TOK:
                        w = min(512, TOK - a) if a >= OWN else min(512, OWN - a)
                        pieces.append((a, w))
                        a += w
                    for jh in range(2):
                        for pi in range(0, len(pieces), 2):
                            grp_p = pieces[pi:pi + 2]
                            b = alloc(len(grp_p))
                            fns = []
                            for k in range(NK):
                                for ch, (a, w) in enumerate(grp_p):
                                    fns.append(MM(ps[:, b + ch, 0:w], Ks[:, k, jh * 128:(jh + 1) * 128], hT[:, k, a:a + w],
                                                  k == 0, k == NK - 1))
                            S.emit("pe", fns, reads=[BKs, B_hT[0], B_hT[1]], writes=BB(b, len(grp_p)))
                            for ch, (a, w) in enumerate(grp_p):
                                S.emit("act", ACT(kT[:, jh, a:a + w], ps[:, b + ch, 0:w], AF.Copy), reads=BB(b + ch),
                                       writes=W("k", B_k))
                        ropes.append((lambda a, b_, jh=jh: kT[:, jh, a:b_], k0, TOK - k0, B_k, "k"))
                    for rp in ropes:
                        rope_apply(*rp)
                    vtiles = []
                    for r in range(d):
                        for jb in range(k0 // (128 * d), TOK // (128 * d)):
                            vtiles.append(sl(r + d * 128 * jb, 128, d))
                    nvb = TOK // (128 * d) - k0 // (128 * d)
                    for vi in range(0, len(vtiles), 2):
                        pair = vtiles[vi:vi + 2]
                        b = alloc(1)
                        fns = []
                        for pi, cs in enumerate(pair):
                            for k in range(NK):
                                fns.append(MM(ps[:, b, pi * PW:(pi + 1) * PW], hT[:, k, cs], Vs[:, k, :],
                                              k == 0, k == NK - 1, True))
                        S.emit("pe", fns, reads=[BVs, B_hT[0], B_hT[1]], writes=BB(b))
                        n2 = len(pair) * PW
                        ev[0] += 1
                        dst = vv[:, vi:vi + len(pair), :].rearrange("p a b -> p (a b)")
                        if ev[0] % 2:
                            S.emit("act", ACT(dst, ps[:, b, 0:n2], AF.Copy), reads=BB(b), writes=W("v", B_v))
                        else:
                            S.emit("dve", CP(dst, ps[:, b, 0:n2]), reads=BB(b), writes=W("v", B_v))
                    units = []
                    for jh in range(2):
                        Uh, Lh = Ua[:, jh, :], La[:, jh, :]
                        for u in range(2):
                            if g == 0:
                                qsets = []
                                for n in range(4 * u, 4 * u + 4):
                                    jb = 8 + n
                                    qsets.append((slice(128 * jb, 128 * (jb + 1), 1),
                                                  [(slice(128 * (jb - 1), 128 * jb, 1), jb - 1 - 7),
                                                   (slice(128 * jb, 128 * (jb + 1), 1), jb - 7)]))
                                mk = masks[:, M_A0:M_A0 + 1024] if u == 0 else masks[:, M_A1:M_A1 + 1024]
                                dU, dL = Uh[:, u * 512:(u + 1) * 512], Lh[:, u * 512:(u + 1) * 512]
                                pv = lambda bk: ps[:, bk, :]
                                nq = 128
                            elif g == 1:
                                qsets = []
                                for r in (2 * u, 2 * u + 1):
                                    for jb in (2, 3):
                                        cs = lambda jj, r=r: sl(r + 512 * jj, 128, 4)
                                        qsets.append((cs(jb), [(cs(jb - 1), r * nvb + jb - 2), (cs(jb), r * nvb + jb - 1)]))
                                mk = masks[:, M_G1:M_G1 + 1024]
                                rr = lambda X: X.rearrange("p (l r) -> p r l", r=4)[:, 2 * u:2 * u + 2, :]
                                dU, dL = rr(Uh), rr(Lh)
                                pv = lambda bk: ps[:, bk, :].rearrange("p (r l) -> p r l", r=2)
                                nq = 128
                            else:
                                qsets = []
                                for r in range(8 * u, 8 * u + 8):
                                    qsets.append((sl(r + 16 * 64, 64, 16), [(sl(r, 128, 16), r)]))
                                mk = masks[:, M_G2:M_G2 + 512]
                                rr = lambda X: X.rearrange("p (l r) -> p r l", r=16)[:, 8 * u:8 * u + 8, :]
                                dU, dL = rr(Uh), rr(Lh)
                                pv = lambda bk: ps[:, bk, :].rearrange("p (r l) -> p r l", r=8)
                                nq = 64
                            units.append((jh, g, qsets, mk, nq, dU, dL, pv))
                    sts = [None] * len(units)
                    sts[0] = attn_st(0, *units[0])
                    for ui in range(len(units)):
                        if ui + 1 < len(units):
                            sts[ui + 1] = attn_st((ui + 1) % 2, *units[ui + 1])
                        attn_pv(sts[ui])
                    release()
                AGs, BAGs = next_panel()
                for jh in range(2):
                    b = proj_job(AGs, BAGs, jh, lambda k, ch: hT[:, k, OWN + ch * 512:OWN + (ch + 1) * 512], NK, 2,
                                 [B_hT[1]])
                    S.emit("act", ACT(Gt, PS(b, 2), AF.Silu), reads=BB(b, 2), writes=[B_rt])
                    S.emit("dve", lambda e, jh=jh: e.reciprocal(out=La[:, jh, :], in_=La[:, jh, :]), reads=[B_L],
                           writes=[B_L])
                    S.emit("dve", TT(Ua[:, jh, :], Ua[:, jh, :], La[:, jh, :], ALU.mult), reads=[B_U, B_L], writes=[B_U])
                    S.emit("dve", TT(ogT[:, 2 * hp + jh, :], Ua[:, jh, :], Gt, ALU.mult), reads=[B_U, B_rt],
                           writes=[B_og])
                release()

            if stop < 3:
                raise _Stop()
            o = 0
            sg = [carve(o + i * 4096, [128, OWN], F32) for i in range(4)]
            B_sg = [Buf("sg%d" % i) for i in range(4)]
            ph2 = [B_q, B_k, B_v] + B_p2 + [B_U, B_L, B_rt]
            first3 = [True] * 4
            B_mg = B_hT[0]

            def W3(i):
                if first3[i]:
                    first3[i] = False
                    return [B_sg[i]] + ph2
                return [B_sg[i]]

            own_rhs = lambda k, ch: hT[:, k, OWN + ch * 512:OWN + (ch + 1) * 512]
            for mp in range(8):
                GRs, BGRs = next_panel()
                WRs, BWRs = next_panel()
                GAs, BGAs = next_panel()
                WAs, BWAs = next_panel()
                for j in range(2):
                    m = 2 * mp + j
                    s0, s1 = (0, 1) if m % 2 == 0 else (2, 3)
                    b = proj_job(GRs, BGRs, j, own_rhs, NK, 2, [B_hT[1]])
                    S.emit("act", ACT(sg[s0], PS(b, 2), AF.Sigmoid), reads=BB(b, 2), writes=W3(s0))
                    b = proj_job(WRs, BWRs, j, lambda k, ch: hgT[:, k, ch * 512:(ch + 1) * 512], NK, 2, [B_hg])
                    for ch in range(2):
                        S.emit("dve", TT(sg[s0][:, ch * 512:(ch + 1) * 512], ps[:, b + ch, :],
                                         sg[s0][:, ch * 512:(ch + 1) * 512], ALU.mult),
                               reads=BB(b + ch) + [B_sg[s0]], writes=[B_sg[s0]])
                    b = proj_job(GAs, BGAs, j, own_rhs, NK, 2, [B_hT[1]])
                    S.emit("act", ACT(sg[s1], PS(b, 2), AF.Sigmoid), reads=BB(b, 2), writes=W3(s1))
                    b = proj_job(WAs, BWAs, j, lambda k, ch: ogT[:, k, ch * 512:(ch + 1) * 512], 8, 2, [B_og])
                    for ch in range(2):
                        S.emit("dve", TT(sg[s1][:, ch * 512:(ch + 1) * 512], ps[:, b + ch, :],
                                         sg[s1][:, ch * 512:(ch + 1) * 512], ALU.mult),
                               reads=BB(b + ch) + [B_sg[s1]], writes=[B_sg[s1]])
                    S.emit("dve", TT(hT[:, m, 0:OWN], sg[s0], sg[s1], ALU.add), reads=[B_sg[s0], B_sg[s1]],
                           writes=[B_mg])
                release()

            if stop < 4:
                raise _Stop()
            hg_flat = hgT[:].rearrange("p a b -> p (a b)")
            wo_ap, wo_buf = [], []
            for p in range(8):
                if p < 4:
                    apv = hg_flat[:, p * NK * PW:(p + 1) * NK * PW].rearrange("p (k n) -> p k n", n=PW)
                    bf = B_hg
                    S.emit("pool", DMA(apv, w_o[:, p * PW:(p + 1) * PW].rearrange("(k p) n -> p k n", p=128)),
                           writes=[bf], dma="d_wo")
                else:
                    s = p - 4
                    apv = slot_ap(s)
                    bf = B_slot[s]
                    S.emit("pool", DMA(apv, w_o[:, p * PW:(p + 1) * PW].rearrange("(k p) n -> p k n", p=128)),
                           writes=[bf], dma="d_w%d" % s)
                wo_ap.append(apv)
                wo_buf.append(bf)
            o = 0
            gpost = carve(o, [128, D], F32); o += 8192
            xo = [carve(o, [128, D], F32)] * 2; o += 8192
            yn = [carve(o + i * 8192, [128, D], F32) for i in range(2)]; o += 16384
            B_gp, B_yn = Buf("gpost"), [Buf("yn0"), Buf("yn1")]
            _bxo = Buf("xo0")
            B_xo = [_bxo, _bxo]
            S.emit("sp", DMA(gpost, gpost_d), writes=[B_gp] + B_sg, dma="d_gp")
            first4 = {"xo0": True, "xo1": True, "yn0": True, "yn1": True}

            def W4(bf):
                if first4[bf.name]:
                    first4[bf.name] = False
                    return [bf] + B_sg
                return [bf]

            ss2 = stat[:, 0:8]
            rs2 = stat[:, 16:24]
            for t in range(8):
                i = t % 2
                S.emit("sp", DMA(xo[i], x_in[OWN + t * 128:OWN + (t + 1) * 128, :]), writes=W4(B_xo[i]),
                       dma="d_x0")
                b = alloc(4)
                fns = []
                for k in range(NK):
                    for p in range(8):
                        fns.append(MM(ps[:, b + p // 2, (p % 2) * PW:(p % 2 + 1) * PW], hT[:, k, t * 128:(t + 1) * 128],
                                      wo_ap[p][:, k, :], k == 0 and p % 2 == 0, k == NK - 1, True))
                S.emit("pe", fns, reads=[B_mg] + list({id(x): x for x in wo_buf}.values()), writes=BB(b, 4))
                S.emit("act", ACT(yn[i][:, 0:1024], PS(b, 2), AF.Square, accum=ss2[:, t:t + 1]), reads=BB(b, 4),
                       writes=W4(B_yn[i]) + [B_stat])
                S.emit("act", ACT(yn[i][:, 1024:2048], PS(b + 2, 2), AF.Square, accum=stat[:, 8 + t:9 + t]), reads=BB(b, 4),
                       writes=[B_yn[i], B_stat])
                S.emit("dve", TT(ss2[:, t:t + 1], ss2[:, t:t + 1], stat[:, 8 + t:9 + t], ALU.add), reads=[B_stat],
                       writes=[B_stat])
                S.emit("act", ACT(rs2[:, t:t + 1], ss2[:, t:t + 1], AF.Sqrt, bias=EPS, scale=1.0 / D), reads=[B_stat],
                       writes=[B_stat])
                S.emit("dve", lambda e, t=t: e.reciprocal(out=rs2[:, t:t + 1], in_=rs2[:, t:t + 1]), reads=[B_stat],
                       writes=[B_stat])
                S.emit("act", ACT(yn[i][:, 0:1024], PS(b, 2), AF.Copy, scale=rs2[:, t:t + 1]), reads=BB(b, 4) + [B_stat],
                       writes=[B_yn[i]])
                S.emit("act", ACT(yn[i][:, 1024:2048], PS(b + 2, 2), AF.Copy, scale=rs2[:, t:t + 1]),
                       reads=BB(b, 4) + [B_stat], writes=[B_yn[i]])
                S.emit("dve", TT(yn[i], yn[i], gpost, ALU.mult), reads=[B_yn[i], B_gp], writes=[B_yn[i]])
                S.emit("dve", TT(yn[i], yn[i], xo[i], ALU.add), reads=[B_yn[i], B_xo[i]], writes=[B_yn[i]])
                S.emit("sp", DMA(y_out[t * 128:(t + 1) * 128, :], yn[i]), reads=[B_yn[i]], dma="d_y%d" % i)


        try:
          _build_phases()
        except _Stop:
          pass
        if stop < 4:
            S.emit("sp", DMA(y_out[0:128, :], tmp[:, 0:D]), dma="d_y0", force=True)
        if debug:
            B_dbg = Buf("dbg")
            S.emit("sp", DMA(dbg["hT"], hT[:].rearrange("p a b -> p (a b)")), reads=B_hT, dma="d_dbg")
            S.emit("sp", DMA(dbg["ogT"], ogT[:].rearrange("p a b -> p (a b)")), reads=[B_og], dma="d_dbg")
            S.emit("sp", DMA(dbg["hgT"], hgT[:].rearrange("p a b -> p (a b)")), reads=[B_hg], dma="d_dbg")

        if os.environ.get("KLOG"):
            for l in S.log:
                print("OP", l)
        final_waits = [(k, v) for k, v in S.cnt.items() if k.startswith("d_y") or k == "d_dbg"]

        sems = {n: es.enter_context(nc.semaphore(n)) for n in sorted(S.semnames)}
        with nc.Block() as block:
            @block.tensor
            def _(e):
                S.replay("pe", e, sems)

            @block.scalar
            def _(e):
                S.replay("act", e, sems)

            @block.vector
            def _(e):
                S.replay("dve", e, sems)

            @block.gpsimd
            def _(e):
                S.replay("pool", e, sems)

            @block.sync
            def _(e):
                S.replay("sp", e, sems)
                for k, v in final_waits:
                    e.wait_ge(sems[k], v)
    return nc


def _host_inputs(inputs):
    x = np.asarray(inputs["x"], dtype=np.float32)
    f = lambda k: np.ascontiguousarray(np.asarray(inputs[k], dtype=np.float32)[0])
    w_in, w_ro, w_ao, w_o = f("w_in"), f("w_rnn_out"), f("w_attn_out"), f("w_o")
    w_rg, w_ig = f("w_rg"), f("w_ig")
    colT = lambda v: np.ascontiguousarray(v.reshape(16, 128).T)
    conv_w = f("conv_w")
    cw = np.ascontiguousarray(conv_w.reshape(4, 16, 128).transpose(2, 1, 0).reshape(128, 64))
    gpost = np.ascontiguousarray(np.broadcast_to(f("ln_post_g")[None, :], (128, D)))
    bf = ml_dtypes.bfloat16
    cbf = np.zeros((128, 384), dtype=np.float32)
    cbf[:, 0:128] = np.eye(128)
    cbf[:, 128:256] = 1.0
    for i in range(16):
        cbf[i + 16, 256 + i] = -1.0
        cbf[i, 256 + 16 + i] = 1.0
    cbf = cbf.astype(bf)
    k = np.arange(128)[:, None]
    q = np.arange(128)[None, :]
    P = np.where(k >= q, 0.0, NEG).astype(np.float32)
    C = np.where(k <= q, 0.0, NEG).astype(np.float32)
    inv = np.exp(-math.log(500000.0) * np.arange(16, dtype=np.float32) * (2.0 / 32)).astype(np.float32)
    maps = []
    for core in range(8):
        b, half = divmod(core, 2)
        if half == 1:
            xin = np.ascontiguousarray(x[b])
        else:
            xin = np.concatenate([np.zeros((OWN, D), np.float32), x[b, :OWN]], axis=0)
        vecs = np.zeros((128, V_N), np.float32)
        vecs[:, V_GPRE:V_GPRE + 16] = colT(f("ln_pre_g"))
        vecs[:, V_CW:V_CW + 64] = cw
        vecs[:, V_CB:V_CB + 16] = colT(f("conv_b"))
        vecs[:, V_BRG:V_BRG + 16] = colT(f("b_rg"))
        vecs[:, V_BIG:V_BIG + 16] = colT(f("b_ig"))
        vecs[:, V_LAM:V_LAM + 16] = colT(f("lru_lambda"))
        vecs[:, V_HM] = float(half)
        pos = (np.arange(TOK) + (half - 1) * OWN).astype(np.float32)
        ang = pos[None, :] * np.concatenate([inv, inv])[:, None]
        rope = np.concatenate([np.cos(ang), np.sin(ang)], axis=1).astype(np.float32).astype(bf)
        Ph = P if half == 1 else np.full((128, 128), NEG, np.float32)
        mk = np.zeros((128, M_N), np.float32)
        mk[:, M_A0:M_A0 + 1280] = np.concatenate([Ph, C, P, C, P, C, P, C, P, C], axis=1)
        mk[:, M_G1:M_G1 + 1024] = np.concatenate([Ph, C, P, C, Ph, C, P, C], axis=1)
        q64 = np.arange(64)[None, :]
        C2 = np.where((k <= q64 + 64) & ((half == 1) | (k >= 64)), 0.0, NEG).astype(np.float32)
        mk[:, M_G2:M_G2 + 512] = np.tile(C2, (1, 8))
        maps.append({"x_in": xin, "w_in": w_in, "w_ro": w_ro, "w_ao": w_ao, "w_o": w_o, "w_rg": w_rg, "w_ig": w_ig,
                     "vecs": vecs, "gpost": gpost, "cbf": cbf, "rope": np.ascontiguousarray(rope),
                     "masks": mk.astype(bf)})
    return maps


def kernel(**inputs):
    maps = _host_inputs(inputs)
    nc = build_nc(debug=False)
    res = run_bass_kernel_spmd(nc, maps, core_ids=list(range(8)))
    out = np.zeros((4, 2048, D), dtype=np.float32)
    for core in range(8):
        b, half = divmod(core, 2)
        out[b, half * OWN:(half + 1) * OWN] = res.results[core]["y"]
    return out
```
